# Optimizing a Trainium2 kernel written in Bass

```python
import math
import jax, jax.numpy as jnp
from jax import lax
import numpy as np

D_MODEL = 1024
BATCH = 1
SEQ = 16384
DEPTH = 4

HEAD_DIM = 64
RET_HEADS = 4
RET_DK = 64
RET_DV = 128
DIL_HEADS = 4
SB_HEADS = 4
MIX_WIDTH = RET_HEADS * RET_DV + DIL_HEADS * HEAD_DIM + SB_HEADS * HEAD_DIM
IN_SIZES = (RET_HEADS * RET_DK, RET_HEADS * RET_DK, RET_HEADS * RET_DV, RET_HEADS * RET_DV,
            DIL_HEADS * HEAD_DIM, DIL_HEADS * HEAD_DIM, DIL_HEADS * HEAD_DIM,
            SB_HEADS * HEAD_DIM, SB_HEADS * HEAD_DIM, SB_HEADS * HEAD_DIM)
IN_WIDTH = sum(IN_SIZES)
D_FF = 2816
BLOCK = 128
RET_CHUNK = 128
WINDOWS = (128, 512, 2048)
DILATIONS = (1, 4, 16)
ROPE_THETA = 10000.0
ALPHA = (2.0 * DEPTH) ** 0.25
BETA = (8.0 * DEPTH) ** -0.25
LN_EPS = 1e-5
GN_EPS = 1e-6
FFN_RES = 0.5
N_MOD = 9

kernel_name = 'hybrid_retention_dilated_stickbreak_macaron'


def _layer_norm(x, gain, bias):
    xf = x.astype(jnp.float32)
    mu = jnp.mean(xf, -1, keepdims=True)
    var = jnp.mean(jnp.square(xf - mu), -1, keepdims=True)
    y = (xf - mu) * lax.rsqrt(var + LN_EPS)
    return (y * gain.astype(jnp.float32) + bias.astype(jnp.float32)).astype(x.dtype)


def _modulate(x, shift, scale):
    return x * (1.0 + scale) + shift


def _post_norm(x, y, gate, res_w, gain, bias):
    return _layer_norm(ALPHA * x + res_w * (1.0 + gate) * y, gain, bias)


def _swiglu(h, w_gate, w_up, w_down):
    return (jax.nn.silu(h @ w_gate) * (h @ w_up)) @ w_down


def _split_heads(x, n_heads):
    b, s, _ = x.shape
    return x.reshape(b, s, n_heads, -1).transpose(0, 2, 1, 3)


def _merge_heads(x):
    b, h, s, d = x.shape
    return x.transpose(0, 2, 1, 3).reshape(b, s, h * d)


def _rotate(x, inv_freq):
    s, d = x.shape[2], x.shape[3]
    ang = jnp.arange(s, dtype=jnp.float32)[:, None] * inv_freq[None, :]
    cos, sin = jnp.cos(ang), jnp.sin(ang)
    xf = x.astype(jnp.float32)
    x1, x2 = xf[..., : d // 2], xf[..., d // 2:]
    return jnp.concatenate([x1 * cos - x2 * sin, x1 * sin + x2 * cos], -1).astype(x.dtype)


def _retention(q, k, v, g):
    b, h, s, dk = q.shape
    dv = v.shape[-1]
    c = RET_CHUNK
    n = s // c
    log_gamma = jnp.log1p(-jnp.exp2(-5.0 - jnp.arange(h, dtype=jnp.float32)))
    i = jnp.arange(c, dtype=jnp.float32)
    diff = i[:, None] - i[None, :]
    decay_in = jnp.where(diff >= 0, jnp.exp(log_gamma[:, None, None] * jnp.maximum(diff, 0.0)), 0.0)
    xi = jnp.exp(log_gamma[:, None] * (i + 1.0))
    zeta = jnp.exp(log_gamma[:, None] * (c - 1.0 - i))
    gamma_c = jnp.exp(log_gamma * c)
    qc = q.astype(jnp.float32).reshape(b, h, n, c, dk)
    kc = (k.astype(jnp.float32) * dk ** -0.5).reshape(b, h, n, c, dk)
    vc = v.astype(jnp.float32).reshape(b, h, n, c, dv)
    scores = jnp.einsum('bhncd,bhnmd->bhncm', qc, kc) * decay_in[None, :, None]
    o_inner = jnp.einsum('bhncm,bhnme->bhnce', scores, vc)
    kv = jnp.einsum('bhncd,bhnce->bhnde', kc * zeta[None, :, None, :, None], vc)

    def step(state, kv_n):
        return state * gamma_c[None, :, None, None] + kv_n, state

    init = jnp.zeros((b, h, dk, dv), jnp.float32)
    _, prev = lax.scan(step, init, jnp.moveaxis(kv, 2, 0))
    prev = jnp.moveaxis(prev, 0, 2)
    o_cross = jnp.einsum('bhncd,bhnde->bhnce', qc, prev) * xi[None, :, None, :, None]
    o = (o_inner + o_cross).reshape(b, h, s, dv)
    mu = jnp.mean(o, -1, keepdims=True)
    var = jnp.mean(jnp.square(o - mu), -1, keepdims=True)
    o = (o - mu) * lax.rsqrt(var + GN_EPS)
    return jax.nn.silu(g.astype(jnp.float32)) * _merge_heads(o)


def _dilated_attention(q, k, v):
    b, h, s, d = q.shape
    nb = s // BLOCK
    n_keys = max(w // dl for w, dl in zip(WINDOWS, DILATIONS)) + 1
    dist = jnp.array(DILATIONS, jnp.int32)[:, None] * jnp.arange(n_keys, dtype=jnp.int32)[None, :]
    in_win = dist <= jnp.array(WINDOWS, jnp.int32)[:, None]
    qb = (q.astype(jnp.float32) * d ** -0.5).reshape(b, h, nb, BLOCK, d).transpose(2, 0, 1, 3, 4)
    kf = k.astype(jnp.float32)
    vf = v.astype(jnp.float32)

    def block(args):
        q_blk, start = args
        t = start + jnp.arange(BLOCK, dtype=jnp.int32)
        idx = t[:, None, None] - dist[None]
        valid = (idx >= 0) & in_win[None]
        idx = jnp.maximum(idx, 0)
        k_g = jnp.take(kf, idx, axis=2)
        v_g = jnp.take(vf, idx, axis=2)
        sc = jnp.einsum('bhqd,bhqpjd->bhqpj', q_blk, k_g)
        sc = jnp.where(valid, sc, -jnp.inf)
        m = jnp.max(sc, -1)
        e = jnp.exp(sc - m[..., None])
        den = jnp.sum(e, -1)
        o_p = jnp.einsum('bhqpj,bhqpjd->bhqpd', e, v_g) / den[..., None]
        wgt = den * jnp.exp(m - jnp.max(m, -1, keepdims=True))
        wgt = wgt / jnp.sum(wgt, -1, keepdims=True)
        return jnp.einsum('bhqp,bhqpd->bhqd', wgt, o_p)

    out = lax.map(block, (qb, jnp.arange(nb, dtype=jnp.int32) * BLOCK))
    return out.transpose(1, 2, 0, 3, 4).reshape(b, h, s, d)


def _stick_breaking(q, k, v):
    b, h, s, d = q.shape
    nb = s // BLOCK
    qb = (q.astype(jnp.float32) * d ** -0.5).reshape(b, h, nb, BLOCK, d).transpose(2, 0, 1, 3, 4)
    kf = k.astype(jnp.float32)
    vf = v.astype(jnp.float32)
    key_pos = jnp.arange(s, dtype=jnp.int32)

    def block(args):
        q_blk, start = args
        t = start + jnp.arange(BLOCK, dtype=jnp.int32)
        z = jnp.einsum('bhqd,bhsd->bhqs', q_blk, kf)
        causal = key_pos[None, :] < t[:, None]
        log_beta = jax.nn.log_sigmoid(z)
        log_keep = jnp.where(causal, jax.nn.log_sigmoid(-z), 0.0)
        between = lax.cumsum(log_keep, axis=3, reverse=True) - log_keep
        a = jnp.where(causal, jnp.exp(log_beta + between), 0.0)
        return jnp.einsum('bhqs,bhsd->bhqd', a, vf)

    out = lax.map(block, (qb, jnp.arange(nb, dtype=jnp.int32) * BLOCK))
    return out.transpose(1, 2, 0, 3, 4).reshape(b, h, s, d)


def _token_mixer(h, w_in, w_out):
    proj = h @ w_in
    points, acc = [], 0
    for size in IN_SIZES[:-1]:
        acc += size
        points.append(acc)
    rq, rk, rv, rg, dq, dk, dv, sq, sk, sv = jnp.split(proj, points, axis=-1)
    ret_freq = 1.0 / (ROPE_THETA ** jnp.linspace(0.0, 1.0, RET_DK // 2, dtype=jnp.float32))
    rope_freq = 1.0 / (ROPE_THETA ** (jnp.arange(0, HEAD_DIM, 2, dtype=jnp.float32) / HEAD_DIM))
    y_ret = _retention(_rotate(_split_heads(rq, RET_HEADS), ret_freq),
                       _rotate(_split_heads(rk, RET_HEADS), ret_freq),
                       _split_heads(rv, RET_HEADS), rg)
    y_dil = _merge_heads(_dilated_attention(_rotate(_split_heads(dq, DIL_HEADS), rope_freq),
                                            _rotate(_split_heads(dk, DIL_HEADS), rope_freq),
                                            _split_heads(dv, DIL_HEADS)))
    y_sb = _merge_heads(_stick_breaking(_split_heads(sq, SB_HEADS), _split_heads(sk, SB_HEADS),
                                        _split_heads(sv, SB_HEADS)))
    y = jnp.concatenate([y_ret, y_dil, y_sb], -1).astype(h.dtype)
    return y @ w_out


def setup_inputs(seed: int = 0) -> dict:
    key = jax.random.key(seed)
    ks = jax.random.split(key, 16)
    f32 = jnp.float32
    nrm = lambda k, shape, scale: jax.random.normal(k, shape, f32) * scale
    return {
        'x': nrm(ks[0], (BATCH, SEQ, D_MODEL), 1.0),
        'c': nrm(ks[1], (BATCH, D_MODEL), 1.0),
        'w_ada': nrm(ks[2], (DEPTH, D_MODEL, N_MOD * D_MODEL), 0.3 * D_MODEL ** -0.5),
        'b_ada': nrm(ks[3], (DEPTH, N_MOD * D_MODEL), 0.01),
        'ln_gain': 1.0 + nrm(ks[4], (DEPTH, 3, D_MODEL), 0.01),
        'ln_bias': nrm(ks[5], (DEPTH, 3, D_MODEL), 0.01),
        'ffn1_w_gate': nrm(ks[6], (DEPTH, D_MODEL, D_FF), D_MODEL ** -0.5),
        'ffn1_w_up': nrm(ks[7], (DEPTH, D_MODEL, D_FF), D_MODEL ** -0.5),
        'ffn1_w_down': nrm(ks[8], (DEPTH, D_FF, D_MODEL), BETA * D_FF ** -0.5),
        'w_in': nrm(ks[9], (DEPTH, D_MODEL, IN_WIDTH), D_MODEL ** -0.5),
        'w_out': nrm(ks[10], (DEPTH, MIX_WIDTH, D_MODEL), BETA * MIX_WIDTH ** -0.5),
        'ffn2_w_gate': nrm(ks[11], (DEPTH, D_MODEL, D_FF), D_MODEL ** -0.5),
        'ffn2_w_up': nrm(ks[12], (DEPTH, D_MODEL, D_FF), D_MODEL ** -0.5),
        'ffn2_w_down': nrm(ks[13], (DEPTH, D_FF, D_MODEL), BETA * D_FF ** -0.5),
    }


def reference(x, c, w_ada, b_ada, ln_gain, ln_bias, ffn1_w_gate, ffn1_w_up, ffn1_w_down,
              w_in, w_out, ffn2_w_gate, ffn2_w_up, ffn2_w_down):
    cond = jax.nn.silu(c)
    for l in range(DEPTH):
        mod = cond @ w_ada[l] + b_ada[l]
        sh1, sc1, g1, sh2, sc2, g2, sh3, sc3, g3 = jnp.split(mod[:, None, :].astype(x.dtype), N_MOD, axis=-1)
        y = _swiglu(_modulate(x, sh1, sc1), ffn1_w_gate[l], ffn1_w_up[l], ffn1_w_down[l])
        x = _post_norm(x, y, g1, FFN_RES, ln_gain[l, 0], ln_bias[l, 0])
        y = _token_mixer(_modulate(x, sh2, sc2), w_in[l], w_out[l])
        x = _post_norm(x, y, g2, 1.0, ln_gain[l, 1], ln_bias[l, 1])
        y = _swiglu(_modulate(x, sh3, sc3), ffn2_w_gate[l], ffn2_w_up[l], ffn2_w_down[l])
        x = _post_norm(x, y, g3, FFN_RES, ln_gain[l, 2], ln_bias[l, 2])
    return x
```

```python
import contextlib
import math
import numpy as np
import concourse.bass as bass
import concourse.mybir as mybir
from concourse.bass_utils import run_bass_kernel_spmd

F32 = mybir.dt.float32
BF16 = mybir.dt.bfloat16
AF = mybir.ActivationFunctionType
ALU = mybir.AluOpType

NCORES = 8
D = 1024
S = 16384
DEPTH = 4
DFF = 2816
NFC = DFF // 128
KC = D // 128
TOK = S // NCORES
NBLK = TOK // 128
TT = 512
NTT = TOK // TT
ALPHA = (2.0 * DEPTH) ** 0.25
LN_EPS = 1e-5
GN_EPS = 1e-6
NDS = 8


class Res:
    __slots__ = ("name", "w", "rd")

    def __init__(self, name):
        self.name = name
        self.w = None
        self.rd = {}


class K:
    def __init__(self, nc, es):
        self.nc = nc
        self.es = es
        self.engs = {"pe": nc.tensor, "act": nc.scalar, "dve": nc.vector, "pool": nc.gpsimd, "sp": nc.sync}
        self.sem = {e: es.enter_context(nc.semaphore("s_" + e)) for e in ("pe", "act", "dve", "pool")}
        self.cnt = {e: 0 for e in self.sem}
        self.known = {e: {} for e in self.engs}
        self.dq = {}
        for q in ("sp", "pool", "act"):
            sems = [es.enter_context(nc.semaphore(f"d_{q}{i}")) for i in range(NDS)]
            self.dq[q] = dict(sems=sems, n=0)
        self.nins = 0

    def sb(self, name, shape, dt):
        self.nalloc = getattr(self, "nalloc", 0) + 1
        return self.es.enter_context(self.nc.sbuf_tensor(f"{name}_{self.nalloc}", list(shape), dt))

    def ps(self, name, shape, dt=F32):
        return self.es.enter_context(self.nc.psum_tensor(name, list(shape), dt))

    def _wait(self, weng, tok):
        if tok[0] == "e":
            _, e, c = tok
            key = ("e", e)
            val = c
            sem = self.sem[e]
        else:
            _, q, i = tok
            slot = i % NDS
            key = ("d", q, slot)
            val = 16 * (i // NDS + 1)
            sem = self.dq[q]["sems"][slot]
        if self.known[weng].get(key, 0) >= val:
            return
        self.engs[weng].wait_ge(sem, val)
        self.known[weng][key] = val

    def _deps(self, eng, reads, writes):
        deps = []
        for r in reads:
            if r.w is not None:
                deps.append(r.w)
        inorder = lambda t: t[0] == "e" and t[1] == eng and eng != "pool"
        for r in writes:
            if r.w is not None and not inorder(r.w):
                deps.append(r.w)
            for t in r.rd.values():
                if not inorder(t):
                    deps.append(t)
        return deps

    def _commit(self, tok, reads, writes):
        key = tok[:2] if tok[0] == "e" else ("d", tok[1], tok[2] % NDS)
        for r in writes:
            r.w = tok
            r.rd = {}
        for r in reads:
            r.rd[key] = tok

    def op(self, eng, fn, reads=(), writes=()):
        for d in self._deps(eng, reads, writes):
            self._wait(eng, d)
        ins = fn()
        self.cnt[eng] += 1
        ins.then_inc(self.sem[eng], 1)
        self._commit(("e", eng, self.cnt[eng]), reads, writes)
        self.nins += 1
        return ins

    def dma(self, q, out, in_, reads=(), writes=(), **kw):
        Dq = self.dq[q]
        i = Dq["n"]
        Dq["n"] += 1
        if i >= NDS:
            self._wait(q, ("d", q, i - NDS))
        for d in self._deps("__dma__", reads, writes):
            self._wait(q, d)
        ins = self.engs[q].dma_start(out=out, in_=in_, **kw)
        ins.then_inc(Dq["sems"][i % NDS], 16)
        tok = ("d", q, i)
        self._commit(tok, reads, writes)
        self.nins += 1
        return tok

    def finish(self, toks):
        for t in toks:
            self._wait("sp", t)


class Rot:
    def __init__(self, items):
        self.items = items
        self.i = 0

    def next(self):
        it = self.items[self.i % len(self.items)]
        self.i += 1
        return it


def k_barrier(k):
    toks = [("e", e, k.cnt[e]) for e in k.sem if k.cnt[e] > 0]
    for q, Dq in k.dq.items():
        n = Dq["n"]
        for i in range(max(0, n - NDS), n):
            toks.append(("d", q, i))
    for w in k.engs:
        for t in toks:
            if t[0] == "e" and t[1] == w:
                continue
            k._wait(w, t)


class Common:
    def __init__(self, k):
        self.k = k
        self.xT = k.sb("xT", [128, KC, TOK], F32)
        self.r_x = [Res(f"x{t}") for t in range(NTT)]
        self.modall = k.sb("mod", [128, DEPTH, 9, KC], F32)
        self.r_mod = Res("mod")
        self.lngall = k.sb("lng", [128, DEPTH, 3, KC], F32)
        self.lnball = k.sb("lnb", [128, DEPTH, 3, KC], F32)
        self.r_ln = Res("ln")
        self.set_layer(0)
        self.ones = k.sb("ones", [128, 128], BF16)
        self.r_ones = Res("ones")
        self.psb = [k.ps(f"psb{i}", [128, 512]) for i in range(8)]
        self.r_ps = [Res(f"ps{i}") for i in range(8)]
        self.psi = 0

    def bank(self):
        i = self.psi % 8
        self.psi += 1
        return self.psb[i], self.r_ps[i]

    def set_layer(self, l):
        self.mod = self.modall[:, l]
        self.lng = self.lngall[:, l]
        self.lnb = self.lnball[:, l]

    def load_consts(self, d):
        k = self.k
        nl_ = d["lng"].shape[1]
        k.dma("sp", self.lngall[:, 0:nl_], d["lng"], writes=[self.r_ln])
        k.dma("sp", self.lnball[:, 0:nl_], d["lnb"], writes=[self.r_ln])
        k.dma("pool", self.ones[:], d["onesd"], writes=[self.r_ones])

    def load_x(self, xd):
        for tt in range(NTT):
            self.k.dma("sp", self.xT[:, :, tt * TT:(tt + 1) * TT],
                       xd.rearrange("(c p) t -> p c t", p=128)[:, :, tt * TT:(tt + 1) * TT], writes=[self.r_x[tt]])

    def store_x(self, xd):
        toks = []
        for tt in range(NTT):
            toks.append(self.k.dma("sp", xd.rearrange("(c p) t -> p c t", p=128)[:, :, tt * TT:(tt + 1) * TT],
                                   self.xT[:, :, tt * TT:(tt + 1) * TT], reads=[self.r_x[tt]]))
        return toks


class LnWS:
    def __init__(self, k):
        self.z = k.sb("z", [128, KC, TT], F32)
        self.r_z = [Res(f"z{i}") for i in range(KC)]
        self.zb = k.sb("zb", [128, KC, TT], BF16)
        self.r_zb = [Res(f"zb{i}") for i in range(KC)]
        self.zq = k.sb("zq", [128, KC, TT], BF16)
        self.r_zq = [Res(f"zq{i}") for i in range(KC)]
        self.mean = k.sb("mean", [128, TT], F32)
        self.r_mean = Res("mean")
        self.rstd = k.sb("rstd", [128, TT], F32)
        self.r_rstd = Res("rstd")
        self.tmp = k.sb("lntmp", [128, TT], F32)
        self.r_tmp = Res("lntmp")
        self.hT = Rot([(k.sb(f"hT{i}", [128, KC, TT], BF16), Res(f"hT{i}")) for i in range(2)])


class FfnWS:
    def __init__(self, k):
        self.actb = k.sb("actb", [128, NFC, TT], BF16)
        self.r_act = [Res(f"act{j}") for j in range(NFC)]
        self.wgu = Rot([(k.sb(f"wgu{i}", [128, 2, KC, 128], BF16), Res(f"wgu{i}")) for i in range(3)])
        self.wd = Rot([(k.sb(f"wd{i}", [128, NFC, 128], BF16), Res(f"wd{i}")) for i in range(2)])
        self.sg = Rot([(k.sb(f"sg{i}", [128, TT], F32), Res(f"sg{i}")) for i in range(2)])


def emit_modulate(cm, tt, vs, hT, r_h):
    k, nc = cm.k, cm.k.nc
    sl = slice(tt * TT, (tt + 1) * TT)
    for kc in range(KC):
        k.op("dve", lambda kc=kc: nc.vector.tensor_scalar(
            hT[:, kc, :], cm.xT[:, kc, sl], cm.mod[:, vs + 1, kc:kc + 1], cm.mod[:, vs, kc:kc + 1],
            op0=ALU.mult, op1=ALU.add), reads=[cm.r_x[tt], cm.r_mod], writes=[r_h])


def emit_postnorm(cm, ln, tt, vg, lni, get_y):
    k, nc = cm.k, cm.k.nc
    sl = slice(tt * TT, (tt + 1) * TT)
    for dc in range(KC):
        yp, r_y = get_y(dc)
        k.op("dve", lambda dc=dc, yp=yp: nc.vector.scalar_tensor_tensor(
            out=ln.z[:, dc, :], in0=yp, scalar=cm.mod[:, vg, dc:dc + 1], in1=cm.xT[:, dc, sl],
            op0=ALU.mult, op1=ALU.add), reads=[r_y, cm.r_mod, cm.r_x[tt]], writes=[ln.r_z[dc]])
        k.op("act", lambda dc=dc: nc.scalar.copy(ln.zb[:, dc, :], ln.z[:, dc, :]),
             reads=[ln.r_z[dc]], writes=[ln.r_zb[dc]])
        k.op("act", lambda dc=dc: nc.scalar.activation(out=ln.zq[:, dc, :], in_=ln.z[:, dc, :], func=AF.Square),
             reads=[ln.r_z[dc]], writes=[ln.r_zq[dc]])
    p1, r1 = cm.bank()
    p2, r2 = cm.bank()
    for dc in range(KC):
        k.op("pe", lambda dc=dc: nc.tensor.matmul(p1[:], cm.ones[:], ln.zb[:, dc, :], start=(dc == 0), stop=(dc == KC - 1)),
             reads=[ln.r_zb[dc], cm.r_ones], writes=[r1])
    for dc in range(KC):
        k.op("pe", lambda dc=dc: nc.tensor.matmul(p2[:], cm.ones[:], ln.zq[:, dc, :], start=(dc == 0), stop=(dc == KC - 1)),
             reads=[ln.r_zq[dc], cm.r_ones], writes=[r2])
    k.op("act", lambda: nc.scalar.copy(ln.mean[:], p1[:]), reads=[r1], writes=[ln.r_mean])
    k.op("act", lambda: nc.scalar.activation(out=ln.tmp[:], in_=p1[:], func=AF.Square), reads=[r1], writes=[ln.r_tmp])
    k.op("dve", lambda: nc.vector.tensor_tensor(ln.rstd[:], p2[:], ln.tmp[:], op=ALU.subtract),
         reads=[r2, ln.r_tmp], writes=[ln.r_rstd])
    k.op("act", lambda: nc.scalar.activation(out=ln.rstd[:], in_=ln.rstd[:], func=AF.Sqrt, bias=LN_EPS / ALPHA ** 2, scale=1.0),
         reads=[ln.r_rstd], writes=[ln.r_rstd])
    k.op("dve", lambda: nc.vector.reciprocal(ln.rstd[:], ln.rstd[:]), reads=[ln.r_rstd], writes=[ln.r_rstd])
    for dc in range(KC):
        k.op("dve", lambda dc=dc: nc.vector.tensor_tensor(ln.z[:, dc, :], ln.z[:, dc, :], ln.mean[:], op=ALU.subtract),
             reads=[ln.r_z[dc], ln.r_mean], writes=[ln.r_z[dc]])
        k.op("dve", lambda dc=dc: nc.vector.tensor_tensor(ln.z[:, dc, :], ln.z[:, dc, :], ln.rstd[:], op=ALU.mult),
             reads=[ln.r_z[dc], ln.r_rstd], writes=[ln.r_z[dc]])
        k.op("act", lambda dc=dc: nc.scalar.activation(
            out=cm.xT[:, dc, sl], in_=ln.z[:, dc, :], func=AF.Identity,
            bias=cm.lnb[:, lni, dc:dc + 1], scale=cm.lng[:, lni, dc:dc + 1]),
            reads=[ln.r_z[dc], cm.r_ln], writes=[cm.r_x[tt]])


def emit_ffn_phase(cm, wgu_d, wd_d, vs, vg, lni):
    k, nc = cm.k, cm.k.nc
    with contextlib.ExitStack() as es2:
        old = k.es
        k.es = es2
        ln = LnWS(k)
        ws = FfnWS(k)
        for tt in range(NTT):
            hT, r_h = ln.hT.next()
            emit_modulate(cm, tt, vs, hT, r_h)
            for j in range(NFC):
                w, r_wt = ws.wgu.next()
                k.dma("pool", w[:], wgu_d[j], writes=[r_wt])
                pg, rg = cm.bank()
                pu, ru = cm.bank()
                for kc in range(KC):
                    k.op("pe", lambda kc=kc: nc.tensor.matmul(pg[:], w[:, 0, kc, :], hT[:, kc, :], start=(kc == 0), stop=(kc == KC - 1)),
                         reads=[r_wt, r_h], writes=[rg])
                for kc in range(KC):
                    k.op("pe", lambda kc=kc: nc.tensor.matmul(pu[:], w[:, 1, kc, :], hT[:, kc, :], start=(kc == 0), stop=(kc == KC - 1)),
                         reads=[r_wt, r_h], writes=[ru])
                sg, r_sg = ws.sg.next()
                k.op("act", lambda: nc.scalar.activation(out=sg[:], in_=pg[:], func=AF.Silu), reads=[rg], writes=[r_sg])
                k.op("dve", lambda j=j: nc.vector.tensor_tensor(ws.actb[:, j, :], pu[:], sg[:], op=ALU.mult),
                     reads=[ru, r_sg], writes=[ws.r_act[j]])

            def get_y(dc):
                w, r_wt = ws.wd.next()
                k.dma("pool", w[:], wd_d[dc], writes=[r_wt])
                py, ry = cm.bank()
                for fc in range(NFC):
                    k.op("pe", lambda fc=fc: nc.tensor.matmul(py[:], w[:, fc, :], ws.actb[:, fc, :], start=(fc == 0), stop=(fc == NFC - 1)),
                         reads=[r_wt, ws.r_act[fc]], writes=[ry])
                return py[:], ry

            emit_postnorm(cm, ln, tt, vg, lni, get_y)
        k_barrier(k)
        k.es = old


def build_mod():
    nc = bass.Bass("TRN2", target_bir_lowering=False)
    NV = DEPTH * 9
    wd_ = nc.dram_tensor("wada", [NV, 128, D], F32, kind="ExternalInput").ap()
    cd = nc.dram_tensor("crep", [128, D], F32, kind="ExternalInput").ap()
    bd = nc.dram_tensor("bada", [128, NV], F32, kind="ExternalInput").ap()
    od = nc.dram_tensor("modout", [128, NV], F32, kind="ExternalOutput").ap()
    with contextlib.ExitStack() as es:
        k = K(nc, es)
        cond = k.sb("cond", [128, D], F32)
        r_c = Res("cond")
        bt = k.sb("bt", [128, NV], F32)
        r_b = Res("bt")
        res = k.sb("res", [128, NV], F32)
        r_res = Res("res")
        wb = Rot([(k.sb(f"w{i}", [128, D], F32), Res(f"w{i}")) for i in range(3)])
        pr = Rot([(k.sb(f"pr{i}", [128, D], F32), Res(f"pr{i}")) for i in range(2)])
        k.dma("sp", cond[:], cd, writes=[r_c])
        k.dma("sp", bt[:], bd, writes=[r_b])
        k.op("act", lambda: nc.scalar.activation(out=cond[:], in_=cond[:], func=AF.Silu), reads=[r_c], writes=[r_c])
        for i in range(NV):
            w, r_w = wb.next()
            k.dma("sp", w[:], wd_[i], writes=[r_w])
            p, r_p = pr.next()
            k.op("dve", lambda: nc.vector.tensor_tensor(p[:], w[:], cond[:], op=ALU.mult), reads=[r_w, r_c], writes=[r_p])
            k.op("dve", lambda i=i: nc.vector.reduce_sum(res[:, i:i + 1], p[:], axis=mybir.AxisListType.X),
                 reads=[r_p], writes=[r_res])
        k.op("dve", lambda: nc.vector.tensor_tensor(res[:], res[:], bt[:], op=ALU.add), reads=[r_res, r_b], writes=[r_res])
        for l in range(DEPTH):
            for s_, rw in enumerate((0.5, 1.0, 0.5)):
                c1 = l * 9 + s_ * 3 + 1
                c2 = c1 + 1
                k.op("dve", lambda c1=c1: nc.vector.tensor_scalar(res[:, c1:c1 + 1], res[:, c1:c1 + 1], 1.0, None, op0=ALU.add),
                     reads=[r_res], writes=[r_res])
                k.op("dve", lambda c2=c2, rw=rw: nc.vector.tensor_scalar(res[:, c2:c2 + 1], res[:, c2:c2 + 1], 1.0, rw / ALPHA,
                                                                        op0=ALU.add, op1=ALU.mult),
                     reads=[r_res], writes=[r_res])
        t = k.dma("sp", od, res[:], reads=[r_res])
        k.finish([t])
    return nc


def core_tokens(c):
    return (128 * (8 * np.arange(NBLK)[:, None] + c) + np.arange(128)[None, :]).reshape(-1)


def lay_wgu(wg, wu):
    a = wg.reshape(KC, 128, NFC, 128).transpose(2, 1, 0, 3)
    b = wu.reshape(KC, 128, NFC, 128).transpose(2, 1, 0, 3)
    return np.ascontiguousarray(np.stack([a, b], axis=2))


def lay_wd(wd):
    return np.ascontiguousarray(wd.reshape(NFC, 128, KC, 128).transpose(2, 1, 0, 3))


def lay_vec(v):
    return np.ascontiguousarray(v.reshape(-1, 128).T)


def run_mod(c, w_ada, b_ada):
    nc = build_mod()
    in_maps = []
    crep = np.ascontiguousarray(np.broadcast_to(c.reshape(1, D), (128, D)))
    for core in range(NCORES):
        wl = []
        bl = []
        for l in range(DEPTH):
            for v in range(9):
                sl = slice(v * D + core * 128, v * D + core * 128 + 128)
                wl.append(w_ada[l][:, sl].T)
                bl.append(b_ada[l][sl])
        in_maps.append({"wada": np.ascontiguousarray(np.stack(wl)), "crep": crep,
                        "bada": np.ascontiguousarray(np.stack(bl, axis=1))})
    res = run_bass_kernel_spmd(nc, in_maps, core_ids=list(range(NCORES)))
    m = np.stack([res.results[core]["modout"] for core in range(NCORES)], axis=-1)
    return np.ascontiguousarray(m.reshape(128, DEPTH, 9, KC))


GOFF = dict(rq=0, rk=256, rv=512, rg=1024, dq=1536, dk=1792, dv=2048, sq=2304, sk=2560, sv=2816)
NLOC = 20
TMW = 2048
PAIRG = [("rq", h, "h") for h in range(4)] + [("dq", h, "h") for h in range(4)] + \
        [("rk", 0, "p"), ("rk", 1, "p"), ("dk", 0, "p"), ("dk", 1, "p")]
SINGG = [("sq", h, "h") for h in range(4)] + [("sk", 0, "p"), ("sk", 1, "p")] + [("rg", i, "g") for i in range(4)]


def emit_inproj_phase(cm, dd, r_loc=None, r_fm=None, r_tm=None):
    k, nc = cm.k, cm.k.nc
    loc, fm, tm = dd["loc"], dd["fm"], dd["tm"]
    wl = lambda r: [r] if r is not None else []
    with contextlib.ExitStack() as es2:
        old = k.es
        k.es = es2
        hTs = Rot([(k.sb(f"ihT{i}", [128, KC, TT], BF16), Res(f"ihT{i}")) for i in range(2)])
        wpair = Rot([(k.sb(f"wp{i}", [128, 2, KC, 128], BF16), Res(f"wp{i}")) for i in range(3)])
        wtm = Rot([(k.sb(f"wtm{i}", [128, KC, 512], BF16), Res(f"wtm{i}")) for i in range(5)])
        tab = Rot([(k.sb(f"tab{i}", [128, 2, TT], F32), Res(f"tab{i}")) for i in range(3)])
        t1 = Rot([(k.sb(f"t1_{i}", [128, TT], F32), Res(f"t1_{i}")) for i in range(2)])
        t2 = Rot([(k.sb(f"t2_{i}", [128, TT], F32), Res(f"t2_{i}")) for i in range(2)])
        ob = Rot([(k.sb(f"ob{i}", [128, TT], BF16), Res(f"ob{i}")) for i in range(4)])
        tmtab = Rot([(k.sb(f"tmtab{i}", [128, 2, 512], F32), Res(f"tmtab{i}")) for i in range(2)])
        for tt in range(NTT):
            sl = slice(tt * TT, (tt + 1) * TT)
            hT, r_h = hTs.next()
            emit_modulate(cm, tt, 3, hT, r_h)
            for pi, (name, idx, kind) in enumerate(PAIRG):
                w, r_w = wpair.next()
                k.dma("pool", w[:], dd["winp"][pi], writes=[r_w])
                pm, rm = cm.bank()
                psw, rsw = cm.bank()
                for kc in range(KC):
                    k.op("pe", lambda kc=kc: nc.tensor.matmul(pm[:], w[:, 0, kc, :], hT[:, kc, :], start=(kc == 0), stop=(kc == KC - 1)),
                         reads=[r_w, r_h], writes=[rm])
                for kc in range(KC):
                    k.op("pe", lambda kc=kc: nc.tensor.matmul(psw[:], w[:, 1, kc, :], hT[:, kc, :], start=(kc == 0), stop=(kc == KC - 1)),
                         reads=[r_w, r_h], writes=[rsw])
                if name == "rq":
                    variants = [(0, loc[0 + idx], r_loc), (1 + idx // 2, loc[4 + idx], r_loc)]
                elif name == "dq":
                    variants = [(4, loc[8 + idx], r_loc)]
                elif name == "rk":
                    variants = [(3, fm[4 + idx], r_fm)]
                else:
                    variants = [(5, fm[2 + idx], r_fm)]
                for ti, dest, r_d in variants:
                    tb, r_tb = tab.next()
                    k.dma("sp", tb[:], dd["fmtab"][ti][:, :, sl], writes=[r_tb])
                    a, r_a = t1.next()
                    b, r_b = t2.next()
                    o, r_o = ob.next()
                    k.op("dve", lambda a=a, tb=tb: nc.vector.tensor_tensor(a[:], pm[:], tb[:, 0, :], op=ALU.mult),
                         reads=[rm, r_tb], writes=[r_a])
                    k.op("dve", lambda b=b, tb=tb: nc.vector.tensor_tensor(b[:], psw[:], tb[:, 1, :], op=ALU.mult),
                         reads=[rsw, r_tb], writes=[r_b])
                    k.op("pool", lambda a=a, b=b, o=o: nc.gpsimd.tensor_tensor(o[:], a[:], b[:], op=ALU.add),
                         reads=[r_a, r_b], writes=[r_o])
                    k.dma("sp", dest[:, sl], o[:], reads=[r_o], writes=wl(r_d))
            for si, (name, idx, kind) in enumerate(SINGG):
                w, r_w = wpair.next()
                k.dma("pool", w[:, 0], dd["wins"][si], writes=[r_w])
                p, rp = cm.bank()
                for kc in range(KC):
                    k.op("pe", lambda kc=kc: nc.tensor.matmul(p[:], w[:, 0, kc, :], hT[:, kc, :], start=(kc == 0), stop=(kc == KC - 1)),
                         reads=[r_w, r_h], writes=[rp])
                o, r_o = ob.next()
                if name == "sq":
                    k.op("act", lambda o=o: nc.scalar.activation(out=o[:], in_=p[:], func=AF.Identity, scale=0.125),
                         reads=[rp], writes=[r_o])
                    dest, r_d = loc[12 + idx], r_loc
                elif name == "sk":
                    k.op("act", lambda o=o: nc.scalar.copy(o[:], p[:]), reads=[rp], writes=[r_o])
                    dest, r_d = fm[idx], r_fm
                else:
                    k.op("act", lambda o=o: nc.scalar.activation(out=o[:], in_=p[:], func=AF.Silu), reads=[rp], writes=[r_o])
                    dest, r_d = loc[16 + idx], r_loc
                k.dma("sp", dest[:, sl], o[:], reads=[r_o], writes=wl(r_d))
            wg_ = []
            for g in range(5):
                w, r_w = wtm.next()
                k.dma("pool", w[:], dd["wintm"][g], writes=[r_w])
                wg_.append((w, r_w))
            for b_ in range(4):
                blk = tt * 4 + b_
                rows = slice(blk * 128, blk * 128 + 128)
                pk = {}
                for g in range(5):
                    w, r_w = wg_[g]
                    p, rp = cm.bank()
                    for kc in range(KC):
                        k.op("pe", lambda kc=kc: nc.tensor.matmul(p[:], hT[:, kc, b_ * 128:(b_ + 1) * 128], w[:, kc, :],
                                                                  start=(kc == 0), stop=(kc == KC - 1)),
                             reads=[r_w, r_h], writes=[rp])
                    if g < 2:
                        pk[g] = (p, rp)
                        if g == 1:
                            tb, r_tb = tmtab.next()
                            k.dma("sp", tb[:], dd["tmtab"][blk], writes=[r_tb])
                            a, r_a = t1.next()
                            b, r_b = t2.next()
                            o, r_o = ob.next()
                            k.op("dve", lambda a=a, tb=tb: nc.vector.tensor_tensor(a[:], pk[0][0][:], tb[:, 0, :], op=ALU.mult),
                                 reads=[pk[0][1], r_tb], writes=[r_a])
                            k.op("dve", lambda b=b, tb=tb: nc.vector.tensor_tensor(b[:], pk[1][0][:], tb[:, 1, :], op=ALU.mult),
                                 reads=[pk[1][1], r_tb], writes=[r_b])
                            k.op("pool", lambda a=a, b=b, o=o: nc.gpsimd.tensor_tensor(o[:], a[:], b[:], op=ALU.add),
                                 reads=[r_a, r_b], writes=[r_o])
                            k.dma("sp", tm[rows, 1024:1536], o[:], reads=[r_o], writes=wl(r_tm))
                    else:
                        o, r_o = ob.next()
                        k.op("act", lambda o=o, p=p: nc.scalar.copy(o[:], p[:]), reads=[rp], writes=[r_o])
                        c0 = {2: 1536, 3: 512, 4: 0}[g]
                        k.dma("sp", tm[rows, c0:c0 + 512], o[:], reads=[r_o], writes=wl(r_tm))
        k_barrier(k)
        k.es = old


def swap_idx(n=256):
    j = np.arange(n)
    return (j // 64) * 64 + (j % 64 + 32) % 64


def lay_win(w_in):
    W = w_in.reshape(KC, 128, -1)
    sw = swap_idx()

    def chunk(cols):
        return W[:, :, cols].transpose(1, 0, 2)

    def head_pad(off, h, swapped):
        out = np.zeros((128, KC, 128), np.float32)
        cols = h * 64 + np.arange(64)
        cols = off + (sw[cols] if swapped else cols)
        out[:, :, (h % 2) * 64:(h % 2) * 64 + 64] = chunk(cols)
        return out

    winp = np.empty((len(PAIRG), 128, 2, KC, 128), np.float32)
    for pi, (name, idx, kind) in enumerate(PAIRG):
        off = GOFF[name]
        if kind == "h":
            winp[pi, :, 0] = head_pad(off, idx, False)
            winp[pi, :, 1] = head_pad(off, idx, True)
        else:
            cols = idx * 128 + np.arange(128)
            winp[pi, :, 0] = chunk(off + cols)
            winp[pi, :, 1] = chunk(off + sw[cols])
    wins = np.empty((len(SINGG), 128, KC, 128), np.float32)
    for si, (name, idx, kind) in enumerate(SINGG):
        off = GOFF[name]
        if kind == "h":
            wins[si] = head_pad(off, idx, False)
        else:
            wins[si] = chunk(off + idx * 128 + np.arange(128))
    wintm = np.zeros((5, 128, KC, 512), np.float32)
    for h in range(4):
        dst = slice(h * 128 + (h % 2) * 64, h * 128 + (h % 2) * 64 + 64)
        cols = h * 64 + np.arange(64)
        wintm[0][:, :, dst] = chunk(GOFF["rk"] + cols)
        wintm[1][:, :, dst] = chunk(GOFF["rk"] + sw[cols])
        wintm[3][:, :, dst] = chunk(GOFF["dv"] + cols)
        wintm[4][:, :, dst] = chunk(GOFF["sv"] + cols)
    wintm[2] = chunk(GOFF["rv"] + np.arange(512))
    return winp, wins, wintm


LOG_GAMMA = np.log1p(-np.exp2(-5.0 - np.arange(4, dtype=np.float64)))


def rope_tables(c):
    t = core_tokens(c).astype(np.float32)
    ret_freq = (1.0 / (np.float32(10000.0) ** np.linspace(0.0, 1.0, 32, dtype=np.float32))).astype(np.float32)
    rope_freq = (1.0 / (np.float32(10000.0) ** (np.arange(0, 64, 2, dtype=np.float32) / np.float32(64)))).astype(np.float32)
    p = np.arange(128)
    d = p % 64
    f = d % 32
    sign = np.where(d < 32, -1.0, 1.0)[:, None]
    hp = p // 64

    def cs(freq):
        ang = (freq[f][:, None] * t[None, :]).astype(np.float32).astype(np.float64)
        return np.cos(ang), sign * np.sin(ang)

    cr, sr = cs(ret_freq)
    cd_, sd = cs(rope_freq)
    r = 128 * c + (np.arange(TOK) % 128)
    fmtab = np.empty((6, 128, 2, TOK), np.float32)
    fmtab[0, :, 0], fmtab[0, :, 1] = cr, sr
    for half in range(2):
        h = 2 * half + hp
        xi = np.exp(LOG_GAMMA[h][:, None] * (r[None, :] + 1.0))
        fmtab[1 + half, :, 0], fmtab[1 + half, :, 1] = cr * xi, sr * xi
    fmtab[3, :, 0], fmtab[3, :, 1] = cr / 8.0, sr / 8.0
    fmtab[4, :, 0], fmtab[4, :, 1] = cd_ / 8.0, sd / 8.0
    fmtab[5, :, 0], fmtab[5, :, 1] = cd_, sd
    col = np.arange(512)
    ch = col // 128
    cdd = col % 64
    cf = cdd % 32
    csign = np.where(cdd < 32, -1.0, 1.0)
    tmtab = np.empty((NBLK, 128, 2, 512), np.float32)
    for i in range(NBLK):
        tpos = (128 * (8 * i + c) + np.arange(128)).astype(np.float32)
        ang = (tpos[:, None] * ret_freq[cf][None, :]).astype(np.float32).astype(np.float64)
        rr = 128 * c + np.arange(128)
        zk = np.exp(LOG_GAMMA[ch][None, :] * (1023.0 - rr[:, None])) / 8.0
        tmtab[i, :, 0] = np.cos(ang) * zk
        tmtab[i, :, 1] = csign[None, :] * np.sin(ang) * zk
    return fmtab, tmtab


def pipeline(n, stages):
    ns = len(stages)
    for t in range(n + ns - 1):
        for s in reversed(range(ns)):
            b = t - s
            if 0 <= b < n:
                stages[s](b)


class MixCommon:
    def __init__(self, cm):
        k = cm.k
        self.yT = k.sb("yT", [128, KC, TOK], BF16)
        self.r_y = [Res(f"y{i}") for i in range(NBLK)]


def emit_retention(cm, mx, dd, r_in=()):
    k, nc = cm.k, cm.k.nc
    loc, fmg, tmg = dd["loc"], dd["fmg"], dd["tmg"]
    g1024 = [float(np.exp(LOG_GAMMA[h] * 1024.0)) for h in range(4)]
    rin = list(r_in)
    with contextlib.ExitStack() as es2:
        old = k.es
        k.es = es2
        dmask = k.sb("dmask", [128, 8, 512], F32)
        r_dm = Res("dmask")
        for j_ in range(8):
            k.dma("sp", dmask[:, j_, :], dd["dmask"][j_], writes=[r_dm])
        on128 = k.sb("on128", [128, 128], BF16)
        r_on = Res("on128")
        k.dma("pool", on128[:], dd["on128d"], writes=[r_on])
        state = k.sb("rstate", [128, 256], F32)
        r_st = Res("rstate")
        k.op("dve", lambda: nc.vector.memset(state[:], 0.0), writes=[r_st])
        stb = Rot([(k.sb(f"stb{i}", [128, 256], BF16), Res(f"stb{i}")) for i in range(2)])
        kv = Rot([(k.sb(f"rkv{i}", [128, 1024], BF16), Res(f"rkv{i}")) for i in range(4)])
        kts = Rot([(k.sb(f"rkt{i}", [128, 2, 128], BF16), Res(f"rkt{i}")) for i in range(4)])
        sms = Rot([(k.sb(f"rsm{i}", [128, 512], BF16), Res(f"rsm{i}")) for i in range(3)])
        qs = Rot([(k.sb(f"rq{i}", [128, 8, 128], BF16), Res(f"rq{i}")) for i in range(2)])
        sgs = Rot([(k.sb(f"rsg{i}", [128, 4, 128], BF16), Res(f"rsg{i}")) for i in range(2)])
        ob = k.sb("rob", [128, 512], BF16)
        r_ob = Res("rob")
        osq = k.sb("rosq", [128, 512], BF16)
        r_osq = Res("rosq")
        mean = k.sb("rmean", [128, 512], F32)
        r_mean = Res("rmean")
        m2 = k.sb("rm2", [128, 512], F32)
        r_m2 = Res("rm2")
        rstd = k.sb("rrstd", [128, 512], F32)
        r_rstd = Res("rrstd")
        tt_ = k.sb("rtt", [128, 512], F32)
        r_tt = Res("rtt")
        Sp = [(cm.psb[i], cm.r_ps[i]) for i in (0, 1)]
        Op, r_O = cm.psb[2], cm.r_ps[2]
        KVp, r_KV = cm.psb[3], cm.r_ps[3]
        Mp, r_M = cm.psb[4], cm.r_ps[4]
        Ep, r_E = cm.psb[5], cm.r_ps[5]
        sb_cur, r_sb_cur = stb.next()
        k.op("dve", lambda: nc.vector.memset(sb_cur[:], 0.0), writes=[r_sb_cur])
        for i in range(NBLK):
            tsl = slice(i * 128, (i + 1) * 128)
            q, r_q = qs.next()
            k.dma("sp", q[:], loc[0:8, :, tsl].rearrange("a p t -> p a t"), reads=rin, writes=[r_q])
            sg, r_sg = sgs.next()
            k.dma("sp", sg[:], loc[16:20, :, tsl].rearrange("a p t -> p a t"), reads=rin, writes=[r_sg])
            held = {}

            def s0(j, tsl=tsl, q=q, r_q=r_q, held=held):
                kvt, r_kv = kv.next()
                k.dma("sp", kvt[:], tmg[j, tsl, 1024:2048], reads=rin, writes=[r_kv])
                kt, r_kt = kts.next()
                k.dma("sp", kt[:], fmg[j, 4:6, :, tsl].rearrange("a p t -> p a t"), reads=rin, writes=[r_kt])
                sp_, r_sp = Sp[j % 2]
                for h in range(4):
                    k.op("pe", lambda h=h: nc.tensor.matmul(
                        sp_[:, h * 128:(h + 1) * 128], kt[:, h // 2, :], q[:, h, :], start=True, stop=True),
                        reads=[r_kt, r_q], writes=[r_sp])
                for h in range(4):
                    k.op("pe", lambda h=h: nc.tensor.matmul(
                        KVp[:, (h // 2) * 128:(h // 2 + 1) * 128], kvt[:, h * 128:(h + 1) * 128],
                        kvt[:, 512 + h * 128:512 + (h + 1) * 128],
                        start=(j == 0 and h == 0), stop=(j == 7), skip_group_check=True), reads=[r_kv], writes=[r_KV])
                held[j] = (kvt, r_kv, sp_, r_sp)

            def s1(j, held=held):
                kvt, r_kv, sp_, r_sp = held[j]
                sm, r_sm = sms.next()
                k.op("dve", lambda: nc.vector.tensor_tensor(sm[:], sp_[:], dmask[:, j, :], op=ALU.mult),
                     reads=[r_sp, r_dm], writes=[r_sm])
                held[j] = (kvt, r_kv, sm, r_sm)

            def s2(j, held=held):
                kvt, r_kv, sm, r_sm = held[j]
                for h in range(4):
                    k.op("pe", lambda h=h: nc.tensor.matmul(
                        Op[:, h * 128:(h + 1) * 128], kvt[:, 512 + h * 128:512 + (h + 1) * 128], sm[:, h * 128:(h + 1) * 128],
                        start=(j == 0 and h == 0), stop=False, skip_group_check=True), reads=[r_kv, r_sm], writes=[r_O])

            pipeline(8, [s0, s1, s2])
            for h in range(4):
                k.op("pe", lambda h=h: nc.tensor.matmul(
                    Op[:, h * 128:(h + 1) * 128], sb_cur[:, (h // 2) * 128:(h // 2 + 1) * 128], q[:, 4 + h, :],
                    start=False, stop=True, skip_group_check=True), reads=[r_sb_cur, r_q], writes=[r_O])
            if i < NBLK - 1:
                for h in range(4):
                    ps_ = slice((h % 2) * 64, (h % 2) * 64 + 64)
                    cs = slice((h // 2) * 128, (h // 2 + 1) * 128)
                    k.op("dve", lambda h=h, ps_=ps_, cs=cs: nc.vector.scalar_tensor_tensor(
                        out=state[ps_, cs], in0=state[ps_, cs], scalar=g1024[h], in1=KVp[ps_, cs], op0=ALU.mult, op1=ALU.add),
                        reads=[r_st, r_KV], writes=[r_st])
                sb_cur, r_sb_cur = stb.next()
                k.op("act", lambda sb_cur=sb_cur: nc.scalar.copy(sb_cur[:], state[:]), reads=[r_st], writes=[r_sb_cur])
            k.op("act", lambda: nc.scalar.copy(ob[:], Op[:]), reads=[r_O], writes=[r_ob])
            k.op("act", lambda: nc.scalar.activation(out=osq[:], in_=Op[:], func=AF.Square), reads=[r_O], writes=[r_osq])
            k.op("pe", lambda: nc.tensor.matmul(Mp[:], on128[:], ob[:], start=True, stop=True), reads=[r_on, r_ob], writes=[r_M])
            k.op("pe", lambda: nc.tensor.matmul(Ep[:], on128[:], osq[:], start=True, stop=True), reads=[r_on, r_osq], writes=[r_E])
            k.op("act", lambda: nc.scalar.copy(mean[:], Mp[:]), reads=[r_M], writes=[r_mean])
            k.op("act", lambda: nc.scalar.activation(out=m2[:], in_=Mp[:], func=AF.Square), reads=[r_M], writes=[r_m2])
            k.op("dve", lambda: nc.vector.tensor_tensor(rstd[:], Ep[:], m2[:], op=ALU.subtract), reads=[r_E, r_m2], writes=[r_rstd])
            k.op("act", lambda: nc.scalar.activation(out=rstd[:], in_=rstd[:], func=AF.Sqrt, bias=GN_EPS, scale=1.0),
                 reads=[r_rstd], writes=[r_rstd])
            k.op("dve", lambda: nc.vector.reciprocal(rstd[:], rstd[:]), reads=[r_rstd], writes=[r_rstd])
            k.op("dve", lambda: nc.vector.tensor_tensor(tt_[:], Op[:], mean[:], op=ALU.subtract), reads=[r_O, r_mean], writes=[r_tt])
            k.op("dve", lambda: nc.vector.tensor_tensor(tt_[:], tt_[:], rstd[:], op=ALU.mult), reads=[r_tt, r_rstd], writes=[r_tt])
            k.op("dve", lambda tsl=tsl, sg=sg: nc.vector.tensor_tensor(
                mx.yT[:, 0:4, tsl], tt_[:].rearrange("p (a t) -> p a t", a=4), sg[:], op=ALU.mult),
                reads=[r_tt, r_sg], writes=[mx.r_y[i]])
        k_barrier(k)
        k.es = old


def emit_dilated(cm, mx, dd, r_in=()):
    k, nc = cm.k, cm.k.nc
    loc, fmg, tmg = dd["loc"], dd["fmg"], dd["tmg"]
    rin = list(r_in)
    with contextlib.ExitStack() as es2:
        old = k.es
        k.es = es2
        mm = k.sb("mmask", [128, 24, 128], F32)
        r_mm = Res("mmask")
        for j_ in range(24):
            k.dma("sp", mm[:, j_, :], dd["mmask"][j_], writes=[r_mm])
        onp = k.sb("onpad", [128, 2, 128], BF16)
        r_on = Res("onpad")
        k.dma("pool", onp[:], dd["onpadd"], writes=[r_on])
        kts = Rot([(k.sb(f"dkt{i}", [128, 2, 128], BF16), Res(f"dkt{i}")) for i in range(4)])
        vs = Rot([(k.sb(f"dv{i}", [128, 512], BF16), Res(f"dv{i}")) for i in range(5)])
        es_ = Rot([(k.sb(f"de{i}", [128, 512], F32), Res(f"de{i}")) for i in range(2)])
        as_ = Rot([(k.sb(f"da{i}", [128, 512], BF16), Res(f"da{i}")) for i in range(3)])
        qs = Rot([(k.sb(f"dq{i}", [128, 4, 128], BF16), Res(f"dq{i}")) for i in range(2)])
        rden = k.sb("drden", [128, 256], F32)
        r_rden = Res("drden")
        Zs = [(cm.psb[i], cm.r_ps[i]) for i in (0, 1, 2)]
        Oa = [(cm.psb[i], cm.r_ps[i]) for i in (3, 4)]
        Da = [(cm.psb[i], cm.r_ps[i]) for i in (5, 6)]
        for i in range(NBLK):
            tsl = slice(i * 128, (i + 1) * 128)
            q, r_q = qs.next()
            k.dma("sp", q[:], loc[8:12, :, tsl].rearrange("a p t -> p a t"), reads=rin, writes=[r_q])
            rels = [r for r in range(24) if 8 * (i - 2) + r >= 0]
            Op, r_O = Oa[i % 2]
            Dp, r_D = Da[i % 2]
            held = {}

            def s0(n, i=i, q=q, r_q=r_q, rels=rels, held=held):
                rel = rels[n]
                kb = 8 * (i - 2) + rel
                rank, lb = kb % 8, kb // 8
                ks = slice(lb * 128, (lb + 1) * 128)
                kt, r_kt = kts.next()
                k.dma("sp", kt[:], fmg[rank, 2:4, :, ks].rearrange("a p t -> p a t"), reads=rin, writes=[r_kt])
                v, r_v = vs.next()
                k.dma("sp", v[:], tmg[rank, ks, 512:1024], reads=rin, writes=[r_v])
                zp, r_z = Zs[n % 3]
                for h in range(4):
                    k.op("pe", lambda h=h: nc.tensor.matmul(
                        zp[:, h * 128:(h + 1) * 128], kt[:, h // 2, :], q[:, h, :], start=True, stop=True),
                        reads=[r_kt, r_q], writes=[r_z])
                held[n] = (zp, r_z, v, r_v, rel)

            def s1(n, held=held):
                zp, r_z, v, r_v, rel = held[n]
                e, r_e = es_.next()
                k.op("act", lambda: nc.scalar.activation(out=e[:], in_=zp[:], func=AF.Exp), reads=[r_z], writes=[r_e])
                a, r_a = as_.next()
                k.op("dve", lambda: nc.vector.tensor_tensor(
                    a[:].rearrange("p (a t) -> p a t", a=4), e[:].rearrange("p (a t) -> p a t", a=4),
                    mm[:, rel, :].unsqueeze(1).to_broadcast([128, 4, 128]), op=ALU.mult),
                    reads=[r_e, r_mm], writes=[r_a])
                held[n] = (a, r_a, v, r_v)

            def s2(n, Op=Op, r_O=r_O, Dp=Dp, r_D=r_D, rels=rels, held=held):
                a, r_a, v, r_v = held[n]
                first, last = (n == 0), (n == len(rels) - 1)
                for h in range(4):
                    cs = slice((h // 2) * 128, (h // 2 + 1) * 128)
                    k.op("pe", lambda h=h, cs=cs: nc.tensor.matmul(
                        Op[:, cs], v[:, h * 128:(h + 1) * 128], a[:, h * 128:(h + 1) * 128],
                        start=(first and h == 0), stop=last, skip_group_check=True), reads=[r_v, r_a], writes=[r_O])
                for h in range(4):
                    cs = slice((h // 2) * 128, (h // 2 + 1) * 128)
                    k.op("pe", lambda h=h, cs=cs: nc.tensor.matmul(
                        Dp[:, cs], onp[:, h % 2, :], a[:, h * 128:(h + 1) * 128],
                        start=(first and h == 0), stop=last, skip_group_check=True), reads=[r_on, r_a], writes=[r_D])
                del held[n]

            pipeline(len(rels), [s0, s1, s2])
            k.op("dve", lambda Dp=Dp: nc.vector.reciprocal(rden[:], Dp[:, 0:256]), reads=[r_D], writes=[r_rden])
            k.op("dve", lambda Op=Op, tsl=tsl: nc.vector.tensor_tensor(
                mx.yT[:, 4:6, tsl], Op[:, 0:256].rearrange("p (a t) -> p a t", a=2),
                rden[:].rearrange("p (a t) -> p a t", a=2), op=ALU.mult),
                reads=[r_O, r_rden], writes=[mx.r_y[i]])
        k_barrier(k)
        k.es = old


def emit_stickbreak(cm, mx, dd, r_in=()):
    k, nc = cm.k, cm.k.nc
    loc, fmg, tmg = dd["loc"], dd["fmg"], dd["tmg"]
    rin = list(r_in)
    with contextlib.ExitStack() as es2:
        old = k.es
        k.es = es2
        m01 = k.sb("sbm01", [128, 8, 128], F32)
        mng = k.sb("sbmng", [128, 8, 128], F32)
        r_mk = Res("sbmask")
        for j_ in range(8):
            k.dma("sp", m01[:, j_, :], dd["sbm01"][j_], writes=[r_mk])
            k.dma("sp", mng[:, j_, :], dd["sbmng"][j_], writes=[r_mk])
        trin = k.sb("trin", [128, 128], F32)
        onen = k.sb("onen", [128, 128], F32)
        r_tr = Res("trin")
        k.dma("sp", trin[:], dd["trind"], writes=[r_tr])
        k.dma("sp", onen[:], dd["onend"], writes=[r_tr])
        kts = Rot([(k.sb(f"skt{i}", [128, 2, 128], BF16), Res(f"skt{i}")) for i in range(4)])
        vs = Rot([(k.sb(f"sv{i}", [128, 512], BF16), Res(f"sv{i}")) for i in range(8)])
        es_ = Rot([(k.sb(f"se{i}", [128, 512], F32), Res(f"se{i}")) for i in range(2)])
        sps = Rot([(k.sb(f"ssp{i}", [128, 512], F32), Res(f"ssp{i}")) for i in range(4)])
        lbs = Rot([(k.sb(f"slb{i}", [128, 512], F32), Res(f"slb{i}")) for i in range(5)])
        args = Rot([(k.sb(f"sar{i}", [128, 512], F32), Res(f"sar{i}")) for i in range(2)])
        as_ = Rot([(k.sb(f"sa{i}", [128, 512], BF16), Res(f"sa{i}")) for i in range(3)])
        Ss = [(k.sb(f"sS{i}", [128, 512], F32), Res(f"sS{i}")) for i in range(2)]
        qs = Rot([(k.sb(f"sq{i}", [128, 4, 128], BF16), Res(f"sq{i}")) for i in range(2)])
        Zs = [(cm.psb[i], cm.r_ps[i]) for i in (0, 1, 2)]
        Ps = [(cm.psb[i], cm.r_ps[i]) for i in (3, 4)]
        Oa = [(cm.psb[i], cm.r_ps[i]) for i in (5, 6)]
        v4 = lambda t: t[:].rearrange("p (a t) -> p a t", a=4)
        for i in range(NBLK):
            tsl = slice(i * 128, (i + 1) * 128)
            q, r_q = qs.next()
            k.dma("sp", q[:], loc[12:16, :, tsl].rearrange("a p t -> p a t"), reads=rin, writes=[r_q])
            nkb = 8 * i + 8
            Op, r_O = Oa[i % 2]
            k.op("pool", lambda: nc.gpsimd.memset(Ss[0][0][:], 0.0), writes=[Ss[0][1]])
            held = {}

            def s0(n, i=i, q=q, r_q=r_q, nkb=nkb, held=held):
                kb = nkb - 1 - n
                rank, lb = kb % 8, kb // 8
                ks = slice(lb * 128, (lb + 1) * 128)
                kt, r_kt = kts.next()
                k.dma("sp", kt[:], fmg[rank, 0:2, :, ks].rearrange("a p t -> p a t"), reads=rin, writes=[r_kt])
                v, r_v = vs.next()
                k.dma("sp", v[:], tmg[rank, ks, 0:512], reads=rin, writes=[r_v])
                zp, r_z = Zs[n % 3]
                for h in range(4):
                    k.op("pe", lambda h=h: nc.tensor.matmul(
                        zp[:, h * 128:(h + 1) * 128], kt[:, h // 2, :], q[:, h, :], start=True, stop=True),
                        reads=[r_kt, r_q], writes=[r_z])
                held[n] = dict(zp=zp, r_z=r_z, v=v, r_v=r_v, j=(kb - 8 * i if kb >= 8 * i else None))

            def s1(n, nkb=nkb, held=held):
                H = held[n]
                e, r_e = es_.next()
                k.op("act", lambda: nc.scalar.activation(out=e[:], in_=H["zp"][:], func=AF.Exp), reads=[H["r_z"]], writes=[r_e])
                sp_, r_sp = sps.next()
                k.op("act", lambda: nc.scalar.activation(out=sp_[:], in_=e[:], func=AF.Ln, bias=1.0, scale=1.0),
                     reads=[r_e], writes=[r_sp])
                lb_, r_lb = lbs.next()
                k.op("dve", lambda: nc.vector.tensor_tensor(lb_[:], H["zp"][:], sp_[:], op=ALU.subtract),
                     reads=[H["r_z"], r_sp], writes=[r_lb])
                if H["j"] is not None:
                    k.op("dve", lambda: nc.vector.tensor_tensor(
                        v4(sp_), v4(sp_), m01[:, H["j"], :].unsqueeze(1).to_broadcast([128, 4, 128]), op=ALU.mult),
                        reads=[r_sp, r_mk], writes=[r_sp])
                So, r_So = Ss[n % 2]
                Sn, r_Sn = Ss[(n + 1) % 2]
                if n < nkb - 1:
                    k.op("pool", lambda: nc.gpsimd.tensor_tensor(Sn[:], So[:], sp_[:], op=ALU.add),
                         reads=[r_So, r_sp], writes=[r_Sn])
                H.update(sp=sp_, r_sp=r_sp, lb=lb_, r_lb=r_lb, S=So, r_S=r_So)

            def s2(n, held=held):
                H = held[n]
                pp, r_p = Ps[n % 2]
                k.op("pe", lambda: nc.tensor.matmul(pp[:], trin[:], H["sp"][:], start=True, stop=(n == 0)),
                     reads=[r_tr, H["r_sp"]], writes=[r_p])
                if n > 0:
                    k.op("pe", lambda: nc.tensor.matmul(pp[:], onen[:], H["S"][:], start=False, stop=True),
                         reads=[r_tr, H["r_S"]], writes=[r_p])
                H.update(pp=pp, r_p=r_p)

            def s3(n, held=held):
                H = held[n]
                ar, r_ar = args.next()
                k.op("dve", lambda: nc.vector.tensor_tensor(ar[:], H["pp"][:], H["lb"][:], op=ALU.add),
                     reads=[H["r_p"], H["r_lb"]], writes=[r_ar])
                if H["j"] is not None:
                    k.op("dve", lambda: nc.vector.tensor_tensor(
                        v4(ar), v4(ar), mng[:, H["j"], :].unsqueeze(1).to_broadcast([128, 4, 128]), op=ALU.add),
                        reads=[r_ar, r_mk], writes=[r_ar])
                a, r_a = as_.next()
                k.op("act", lambda: nc.scalar.activation(out=a[:], in_=ar[:], func=AF.Exp), reads=[r_ar], writes=[r_a])
                H.update(a=a, r_a=r_a)

            def s4(n, Op=Op, r_O=r_O, nkb=nkb, held=held):
                H = held[n]
                for h in range(4):
                    cs = slice((h // 2) * 128, (h // 2 + 1) * 128)
                    k.op("pe", lambda h=h, cs=cs: nc.tensor.matmul(
                        Op[:, cs], H["v"][:, h * 128:(h + 1) * 128], H["a"][:, h * 128:(h + 1) * 128],
                        start=(n == 0 and h == 0), stop=(n == nkb - 1), skip_group_check=True),
                        reads=[H["r_v"], H["r_a"]], writes=[r_O])
                del held[n]

            pipeline(nkb, [s0, s1, s2, s3, s4])
            k.op("act", lambda Op=Op, tsl=tsl: nc.scalar.copy(
                mx.yT[:, 6:8, tsl], Op[:, 0:256].rearrange("p (a t) -> p a t", a=2)), reads=[r_O], writes=[mx.r_y[i]])
        k_barrier(k)
        k.es = old


def emit_outproj_phase(cm, mx, wo_d):
    k, nc = cm.k, cm.k.nc
    with contextlib.ExitStack() as es2:
        old = k.es
        k.es = es2
        ln = LnWS(k)
        wo = k.sb("wo", [128, KC, D], BF16)
        r_wo = Res("wo")
        k.dma("pool", wo[:], wo_d, writes=[r_wo])
        for tt in range(NTT):
            sl = slice(tt * TT, (tt + 1) * TT)

            def get_y(dc, tt=tt, sl=sl):
                py, ry = cm.bank()
                for kc in range(KC):
                    k.op("pe", lambda kc=kc: nc.tensor.matmul(py[:], wo[:, kc, dc * 128:(dc + 1) * 128], mx.yT[:, kc, sl],
                                                              start=(kc == 0), stop=(kc == KC - 1)),
                         reads=[r_wo] + mx.r_y[tt * 4:tt * 4 + 4], writes=[ry])
                return py[:], ry

            emit_postnorm(cm, ln, tt, 5, 1, get_y)
        k_barrier(k)
        k.es = old


def mixer_masks(c):
    kk = np.arange(128)[:, None]
    qq = np.arange(128)[None, :]
    dmask = np.zeros((8, 128, 4, 128), np.float64)
    for j in range(8):
        delta = 128 * (c - j) + qq - kk
        for h in range(4):
            dmask[j, :, h, :] = np.where(delta >= 0, np.exp(LOG_GAMMA[h] * np.maximum(delta, 0)), 0.0)
    mmask = np.zeros((24, 128, 128), np.float64)
    for rel in range(24):
        delta = 128 * (16 + c - rel) + qq - kk
        for w, dl in zip((128, 512, 2048), (1, 4, 16)):
            mmask[rel] += ((delta >= 0) & (delta % dl == 0) & (delta <= w))
    m01 = np.zeros((8, 128, 128), np.float32)
    for j in range(8):
        if j < c:
            m01[j] = 1.0
        elif j == c:
            m01[j] = (kk < qq)
    mng = np.where(m01 > 0, 0.0, -1.0e4).astype(np.float32)
    return (dmask.reshape(8, 128, 512).astype(np.float32), mmask.astype(np.float32), m01, mng)


def mixer_consts():
    onpad = np.zeros((128, 2, 128), np.float32)
    onpad[:, 0, 0:64] = 1.0
    onpad[:, 1, 64:128] = 1.0
    trin = np.where(np.arange(128)[:, None] > np.arange(128)[None, :], -1.0, 0.0).astype(np.float32)
    return dict(on128d=np.full((128, 128), 1.0 / 128, np.float32), onpadd=onpad, trind=trin,
                onend=np.full((128, 128), -1.0, np.float32))


NVC = 8


def emit_mod_phase(cm, dd, layers):
    k, nc = cm.k, cm.k.nc
    with contextlib.ExitStack() as es2:
        old = k.es
        k.es = es2
        cond = k.sb("cond", [128, D], F32)
        r_c = Res("cond")
        bt = k.sb("bt", [128, len(layers), 9, KC], F32)
        r_b = Res("bt")
        wb = Rot([(k.sb(f"wa{i}", [128, D], F32), Res(f"wa{i}")) for i in range(3)])
        pr = Rot([(k.sb(f"pr{i}", [128, D], F32), Res(f"pr{i}")) for i in range(2)])
        k.dma("sp", cond[:], dd["crep"], writes=[r_c])
        k.dma("sp", bt[:], dd["bada"], writes=[r_b])
        k.op("act", lambda: nc.scalar.activation(out=cond[:], in_=cond[:], func=AF.Silu), reads=[r_c], writes=[r_c])
        k.op("dve", lambda: nc.vector.memset(cm.modall[:], 0.0), writes=[cm.r_mod])
        for l in layers:
            for v in range(9):
                for kc in range(KC):
                    w, r_w = wb.next()
                    k.dma("sp", w[:], dd["wada"][(l * 9 + v) * KC + kc], writes=[r_w])
                    p, r_p = pr.next()
                    k.op("dve", lambda w=w, p=p: nc.vector.tensor_tensor(p[:], w[:], cond[:], op=ALU.mult),
                         reads=[r_w, r_c], writes=[r_p])
                    k.op("dve", lambda p=p, l=l, v=v, kc=kc: nc.vector.reduce_sum(
                        cm.modall[:, l, v, kc:kc + 1], p[:], axis=mybir.AxisListType.X), reads=[r_p], writes=[cm.r_mod])
        nl_ = len(layers)
        k.op("dve", lambda: nc.vector.tensor_tensor(cm.modall[:, 0:nl_], cm.modall[:, 0:nl_], bt[:], op=ALU.add),
             reads=[cm.r_mod, r_b], writes=[cm.r_mod])
        for l in layers:
            for s_, rw in enumerate((0.5, 1.0, 0.5)):
                v1, v2 = s_ * 3 + 1, s_ * 3 + 2
                k.op("dve", lambda l=l, v1=v1: nc.vector.tensor_scalar(
                    cm.modall[:, l, v1, :], cm.modall[:, l, v1, :], 1.0, None, op0=ALU.add), reads=[cm.r_mod], writes=[cm.r_mod])
                k.op("dve", lambda l=l, v2=v2, rw=rw: nc.vector.tensor_scalar(
                    cm.modall[:, l, v2, :], cm.modall[:, l, v2, :], 1.0, rw / ALPHA, op0=ALU.add, op1=ALU.mult),
                    reads=[cm.r_mod], writes=[cm.r_mod])
        k_barrier(k)
        k.es = old


def in_specs(nlw):
    return (
        ("xin", [NVC, D, TOK], F32), ("crep", [128, D], F32), ("wada", [nlw * 9 * KC, 128, D], F32),
        ("bada", [128, nlw, 9, KC], F32), ("lng", [128, nlw, 3, KC], F32), ("lnb", [128, nlw, 3, KC], F32),
        ("onesd", [128, 128], F32),
        ("wgu1", [nlw, NFC, 128, 2, KC, 128], F32), ("wd1", [nlw, KC, 128, NFC, 128], F32),
        ("wgu2", [nlw, NFC, 128, 2, KC, 128], F32), ("wd2", [nlw, KC, 128, NFC, 128], F32),
        ("winp", [nlw, len(PAIRG), 128, 2, KC, 128], F32), ("wins", [nlw, len(SINGG), 128, KC, 128], F32),
        ("wintm", [nlw, 5, 128, KC, 512], F32), ("wout", [nlw, 128, KC, D], F32),
        ("fmtab", [NVC, 6, 128, 2, TOK], F32), ("tmtab", [NVC, NBLK, 128, 2, 512], F32),
        ("dmask", [NVC, 8, 128, 512], F32), ("mmask", [NVC, 24, 128, 128], F32),
        ("sbm01", [NVC, 8, 128, 128], F32), ("sbmng", [NVC, 8, 128, 128], F32),
        ("on128d", [128, 128], F32), ("onpadd", [128, 2, 128], F32), ("trind", [128, 128], F32), ("onend", [128, 128], F32),
    )


def build_fused(nlw=DEPTH):
    layers = tuple(range(nlw))
    nc = bass.Bass("TRN2", target_bir_lowering=False)
    dd = {}
    for name, shape, dt in in_specs(nlw):
        dd[name] = nc.dram_tensor("d_" + name, list(shape), dt, kind="ExternalInput").ap()
    xout = nc.dram_tensor("d_xout", [NVC, D, TOK], F32, kind="ExternalOutput").ap()
    xbuf = nc.dram_tensor("i_xbuf", [NVC, D, TOK], F32).ap()
    loc = nc.dram_tensor("i_loc", [NVC, NLOC, 128, TOK], BF16).ap()
    fmg = nc.dram_tensor("i_fmg", [NVC, 6, 128, TOK], BF16).ap()
    tmg = nc.dram_tensor("i_tmg", [NVC, TOK, TMW], BF16).ap()
    with contextlib.ExitStack() as es:
        k = K(nc, es)
        cm = Common(k)
        cm.load_consts(dd)
        emit_mod_phase(cm, dd, layers)
        for li, l in enumerate(layers):
            cm.set_layer(l)
            first, last = (li == 0), (li == len(layers) - 1)
            for vc in range(NVC):
                cm.load_x(dd["xin"][vc] if first else xbuf[vc])
                emit_ffn_phase(cm, dd["wgu1"][l], dd["wd1"][l], 0, 2, 0)
                cm.store_x(xbuf[vc])
                emit_inproj_phase(cm, dict(loc=loc[vc], fm=fmg[vc], tm=tmg[vc], winp=dd["winp"][l], wins=dd["wins"][l],
                                           wintm=dd["wintm"][l], fmtab=dd["fmtab"][vc], tmtab=dd["tmtab"][vc]))
            for vc in range(NVC):
                cm.load_x(xbuf[vc])
                ddv = dict(loc=loc[vc], fmg=fmg, tmg=tmg, dmask=dd["dmask"][vc], mmask=dd["mmask"][vc],
                           sbm01=dd["sbm01"][vc], sbmng=dd["sbmng"][vc], on128d=dd["on128d"], onpadd=dd["onpadd"],
                           trind=dd["trind"], onend=dd["onend"])
                with contextlib.ExitStack() as es_mix:
                    k.es = es_mix
                    mx = MixCommon(cm)
                    emit_retention(cm, mx, ddv)
                    emit_dilated(cm, mx, ddv)
                    emit_stickbreak(cm, mx, ddv)
                    emit_outproj_phase(cm, mx, dd["wout"][l])
                    k.es = es
                emit_ffn_phase(cm, dd["wgu2"][l], dd["wd2"][l], 6, 8, 2)
                cm.store_x(xout[vc] if last else xbuf[vc])
                k_barrier(k)
        k_barrier(k)
        build_fused.nops = k.nins
    return nc


def host_inputs(x, c, w_ada, b_ada, ln_gain, ln_bias, ffn1_w_gate, ffn1_w_up, ffn1_w_down,
                w_in, w_out, ffn2_w_gate, ffn2_w_up, ffn2_w_down):
    f32 = lambda a: np.ascontiguousarray(np.asarray(a, dtype=np.float32))
    m = {}
    toks = [core_tokens(vc) for vc in range(NVC)]
    x0 = f32(x)[0]
    m["xin"] = np.ascontiguousarray(np.stack([x0[toks[vc]].T for vc in range(NVC)]))
    m["crep"] = np.ascontiguousarray(np.broadcast_to(f32(c).reshape(1, D), (128, D)))
    wa = f32(w_ada)
    m["wada"] = np.ascontiguousarray(wa.reshape(DEPTH, D, 9 * KC, 128).transpose(0, 2, 3, 1).reshape(DEPTH * 9 * KC, 128, D))
    m["bada"] = np.ascontiguousarray(f32(b_ada).reshape(DEPTH, 9, KC, 128).transpose(3, 0, 1, 2))
    m["lng"] = np.ascontiguousarray(f32(ln_gain).reshape(DEPTH, 3, KC, 128).transpose(3, 0, 1, 2))
    m["lnb"] = np.ascontiguousarray(f32(ln_bias).reshape(DEPTH, 3, KC, 128).transpose(3, 0, 1, 2))
    m["onesd"] = np.full((128, 128), 1.0 / D, np.float32)
    m["wgu1"] = np.stack([lay_wgu(f32(ffn1_w_gate[l]), f32(ffn1_w_up[l])) for l in range(DEPTH)])
    m["wd1"] = np.stack([lay_wd(f32(ffn1_w_down[l])) for l in range(DEPTH)])
    m["wgu2"] = np.stack([lay_wgu(f32(ffn2_w_gate[l]), f32(ffn2_w_up[l])) for l in range(DEPTH)])
    m["wd2"] = np.stack([lay_wd(f32(ffn2_w_down[l])) for l in range(DEPTH)])
    wl = [lay_win(f32(w_in[l])) for l in range(DEPTH)]
    m["winp"] = np.stack([w[0] for w in wl])
    m["wins"] = np.stack([w[1] for w in wl])
    m["wintm"] = np.stack([w[2] for w in wl])
    m["wout"] = np.ascontiguousarray(np.stack([f32(w_out[l]).reshape(KC, 128, D).transpose(1, 0, 2) for l in range(DEPTH)]))
    tabs = [rope_tables(vc) for vc in range(NVC)]
    m["fmtab"] = np.stack([t[0] for t in tabs])
    m["tmtab"] = np.stack([t[1] for t in tabs])
    masks = [mixer_masks(vc) for vc in range(NVC)]
    for i, name in enumerate(("dmask", "mmask", "sbm01", "sbmng")):
        m[name] = np.ascontiguousarray(np.stack([mk[i] for mk in masks]))
    m.update(mixer_consts())
    return m, toks


LAYERS_PER_LAUNCH = 1
PER_LAYER = ("wada", "bada", "lng", "lnb", "wgu1", "wd1", "wgu2", "wd2", "winp", "wins", "wintm", "wout")


def kernel(x, c, w_ada, b_ada, ln_gain, ln_bias, ffn1_w_gate, ffn1_w_up, ffn1_w_down,
           w_in, w_out, ffn2_w_gate, ffn2_w_up, ffn2_w_down):
    m, toks = host_inputs(x, c, w_ada, b_ada, ln_gain, ln_bias, ffn1_w_gate, ffn1_w_up, ffn1_w_down,
                          w_in, w_out, ffn2_w_gate, ffn2_w_up, ffn2_w_down)
    nlw = LAYERS_PER_LAUNCH
    nc = build_fused(nlw)
    xcur = m["xin"]
    for l0 in range(0, DEPTH, nlw):
        mm = dict(m, xin=xcur)
        if nlw < DEPTH:
            ls = slice(l0, l0 + nlw)
            mm["wada"] = np.ascontiguousarray(m["wada"][l0 * 9 * KC:(l0 + nlw) * 9 * KC])
            for name in ("bada", "lng", "lnb"):
                mm[name] = np.ascontiguousarray(m[name][:, ls])
            for name in ("wgu1", "wd1", "wgu2", "wd2", "winp", "wins", "wintm", "wout"):
                mm[name] = np.ascontiguousarray(m[name][ls])
        res = run_bass_kernel_spmd(nc, [{"d_" + a: b for a, b in mm.items()}], core_ids=[0])
        xcur = np.asarray(res.results[0]["d_xout"], dtype=np.float32)
    out = np.empty((1, S, D), np.float32)
    for vc in range(NVC):
        out[0][toks[vc]] = xcur[vc].T
    return out
```
